# Optimizing a Trainium2 kernel written in Bass

```python
import jax, jax.numpy as jnp
from jax import lax
import numpy as np

D_MODEL = 2048
BATCH = 2
SEQ = 4096
DEPTH = 2
DEC_BATCH = 16
DEC_SEQ = 64
PAST_LEN = 1024

CHUNK = 64
N_A_LAYERS = DEPTH // 2
N_B_LAYERS = DEPTH - N_A_LAYERS
CONV_WIDTH = 31
N_HEADS = 16
HEAD_DIM = D_MODEL // N_HEADS
D_FF = ((8 * D_MODEL // 3 + 255) // 256) * 256
Q_BLOCK = 128
EPS = 1e-6
FFN_RES = 0.5

kernel_name = "conformer_conv_stickbreak_yoco_step"


def _rmsnorm(x, g):
    xf = x.astype(jnp.float32)
    y = xf * lax.rsqrt(jnp.mean(xf * xf, axis=-1, keepdims=True) + EPS)
    return (y * g.astype(jnp.float32)).astype(x.dtype)


def _layernorm(x, g, b):
    xf = x.astype(jnp.float32)
    mu = jnp.mean(xf, axis=-1, keepdims=True)
    xc = xf - mu
    y = xc * lax.rsqrt(jnp.mean(xc * xc, axis=-1, keepdims=True) + EPS)
    return (y * g.astype(jnp.float32) + b.astype(jnp.float32)).astype(x.dtype)


def _swiglu(x, w_gate, w_up, w_down):
    return (jax.nn.silu(x @ w_gate) * (x @ w_up)) @ w_down


def _conv_module(h, state, pw1_w, pw1_b, dw_w, dw_b, ln_g, ln_b, pw2_w, pw2_b):
    u = jax.nn.glu(h @ pw1_w + pw1_b, axis=-1)
    ext = jnp.concatenate([state.astype(u.dtype), u], axis=1)
    new_state = ext[:, ext.shape[1] - (CONV_WIDTH - 1):]
    c = lax.conv_general_dilated(
        ext, dw_w[:, None, :].astype(ext.dtype), window_strides=(1,), padding='VALID',
        dimension_numbers=('NWC', 'WIO', 'NWC'), feature_group_count=D_MODEL) + dw_b
    c = jax.nn.silu(_layernorm(c, ln_g, ln_b))
    return c @ pw2_w + pw2_b, new_state


def _stick_breaking(q, k, v, past):
    tq = q.shape[1]
    outs = []
    for i in range(0, tq, Q_BLOCK):
        j = min(i + Q_BLOCK, tq)
        kb = k[:, :past + j]
        vb = v[:, :past + j]
        q_pos = past + jnp.arange(i, j)
        k_pos = jnp.arange(past + j)
        mask = k_pos[None, :] < q_pos[:, None]
        z = jnp.einsum('bqhd,bkhd->bhqk', q[:, i:j], kb).astype(jnp.float32) * (HEAD_DIM ** -0.5)
        log_beta = jax.nn.log_sigmoid(z)
        log_keep = jnp.where(mask, log_beta - z, 0.0)
        stick = lax.cumsum(log_keep, axis=3, reverse=True) - log_keep
        a = jnp.where(mask, jnp.exp(log_beta + stick), 0.0)
        outs.append(jnp.einsum('bhqk,bkhd->bqhd', a.astype(vb.dtype), vb))
    return jnp.concatenate(outs, axis=1)


def _trunk(x, conv_states, cache_k, cache_v, weights):
    (ffn1_norm, ffn1_w_gate, ffn1_w_up, ffn1_w_down, mix_norm,
     ffn2_norm, ffn2_w_gate, ffn2_w_up, ffn2_w_down,
     conv_pw1_w, conv_pw1_b, conv_dw_w, conv_dw_b, conv_ln_g, conv_ln_b,
     conv_pw2_w, conv_pw2_b, kv_norm, w_kv, attn_wq, attn_wo, final_norm) = weights
    b_sz, t_len, _ = x.shape
    hk = N_HEADS * HEAD_DIM
    past = 0 if cache_k is None else cache_k.shape[1]
    new_conv = []
    k_new = v_new = k_all = v_all = None
    for l in range(DEPTH):
        if l == N_A_LAYERS:
            kv = _rmsnorm(x, kv_norm) @ w_kv
            k_new = kv[..., :hk].reshape(b_sz, t_len, N_HEADS, HEAD_DIM)
            v_new = kv[..., hk:].reshape(b_sz, t_len, N_HEADS, HEAD_DIM)
            if cache_k is None:
                k_all, v_all = k_new, v_new
            else:
                k_all = jnp.concatenate([cache_k.astype(k_new.dtype), k_new], axis=1)
                v_all = jnp.concatenate([cache_v.astype(v_new.dtype), v_new], axis=1)
        x = x + FFN_RES * _swiglu(_rmsnorm(x, ffn1_norm[l]), ffn1_w_gate[l], ffn1_w_up[l], ffn1_w_down[l])
        h = _rmsnorm(x, mix_norm[l])
        if l < N_A_LAYERS:
            y, st = _conv_module(h, conv_states[l], conv_pw1_w[l], conv_pw1_b[l], conv_dw_w[l],
                                 conv_dw_b[l], conv_ln_g[l], conv_ln_b[l], conv_pw2_w[l], conv_pw2_b[l])
            new_conv.append(st)
        else:
            bl = l - N_A_LAYERS
            q = (h @ attn_wq[bl]).reshape(b_sz, t_len, N_HEADS, HEAD_DIM)
            o = _stick_breaking(q, k_all, v_all, past)
            y = o.reshape(b_sz, t_len, hk) @ attn_wo[bl]
        x = x + y
        x = x + FFN_RES * _swiglu(_rmsnorm(x, ffn2_norm[l]), ffn2_w_gate[l], ffn2_w_up[l], ffn2_w_down[l])
    return _rmsnorm(x, final_norm), jnp.stack(new_conv, axis=0), k_new, v_new


def setup_inputs(seed: int = 0) -> dict:
    key = jax.random.key(seed)
    ks = jax.random.split(key, 32)
    f32 = jnp.float32
    hk = N_HEADS * HEAD_DIM

    def w(k, shape, fan_in):
        return jax.random.normal(k, shape, f32) * (fan_in ** -0.5)

    def gain(k, shape):
        return 1.0 + 0.02 * jax.random.normal(k, shape, f32)

    def bias(k, shape):
        return 0.01 * jax.random.normal(k, shape, f32)

    return {
        "x_prompt": jax.random.normal(ks[0], (BATCH, SEQ, D_MODEL), f32),
        "x_sample": jax.random.normal(ks[1], (DEC_BATCH, DEC_SEQ, D_MODEL), f32),
        "state_conv": 0.5 * jax.random.normal(ks[2], (N_A_LAYERS, DEC_BATCH, CONV_WIDTH - 1, D_MODEL), f32),
        "cache_k": jax.random.normal(ks[3], (DEC_BATCH, PAST_LEN, N_HEADS, HEAD_DIM), f32),
        "cache_v": jax.random.normal(ks[4], (DEC_BATCH, PAST_LEN, N_HEADS, HEAD_DIM), f32),
        "ffn1_norm": gain(ks[5], (DEPTH, D_MODEL)),
        "ffn1_w_gate": w(ks[6], (DEPTH, D_MODEL, D_FF), D_MODEL),
        "ffn1_w_up": w(ks[7], (DEPTH, D_MODEL, D_FF), D_MODEL),
        "ffn1_w_down": w(ks[8], (DEPTH, D_FF, D_MODEL), D_FF),
        "mix_norm": gain(ks[9], (DEPTH, D_MODEL)),
        "ffn2_norm": gain(ks[10], (DEPTH, D_MODEL)),
        "ffn2_w_gate": w(ks[11], (DEPTH, D_MODEL, D_FF), D_MODEL),
        "ffn2_w_up": w(ks[12], (DEPTH, D_MODEL, D_FF), D_MODEL),
        "ffn2_w_down": w(ks[13], (DEPTH, D_FF, D_MODEL), D_FF),
        "conv_pw1_w": w(ks[14], (N_A_LAYERS, D_MODEL, 2 * D_MODEL), D_MODEL),
        "conv_pw1_b": bias(ks[15], (N_A_LAYERS, 2 * D_MODEL)),
        "conv_dw_w": w(ks[16], (N_A_LAYERS, CONV_WIDTH, D_MODEL), CONV_WIDTH),
        "conv_dw_b": bias(ks[17], (N_A_LAYERS, D_MODEL)),
        "conv_ln_g": gain(ks[18], (N_A_LAYERS, D_MODEL)),
        "conv_ln_b": bias(ks[19], (N_A_LAYERS, D_MODEL)),
        "conv_pw2_w": w(ks[20], (N_A_LAYERS, D_MODEL, D_MODEL), D_MODEL),
        "conv_pw2_b": bias(ks[21], (N_A_LAYERS, D_MODEL)),
        "kv_norm": gain(ks[22], (D_MODEL,)),
        "w_kv": w(ks[23], (D_MODEL, 2 * hk), D_MODEL),
        "attn_wq": w(ks[24], (N_B_LAYERS, D_MODEL, hk), D_MODEL),
        "attn_wo": w(ks[25], (N_B_LAYERS, hk, D_MODEL), hk),
        "final_norm": gain(ks[26], (D_MODEL,)),
    }


def reference(x_prompt, x_sample, state_conv, cache_k, cache_v,
              ffn1_norm, ffn1_w_gate, ffn1_w_up, ffn1_w_down, mix_norm,
              ffn2_norm, ffn2_w_gate, ffn2_w_up, ffn2_w_down,
              conv_pw1_w, conv_pw1_b, conv_dw_w, conv_dw_b, conv_ln_g, conv_ln_b,
              conv_pw2_w, conv_pw2_b, kv_norm, w_kv, attn_wq, attn_wo, final_norm):
    weights = (ffn1_norm, ffn1_w_gate, ffn1_w_up, ffn1_w_down, mix_norm,
               ffn2_norm, ffn2_w_gate, ffn2_w_up, ffn2_w_down,
               conv_pw1_w, conv_pw1_b, conv_dw_w, conv_dw_b, conv_ln_g, conv_ln_b,
               conv_pw2_w, conv_pw2_b, kv_norm, w_kv, attn_wq, attn_wo, final_norm)
    zero_conv = jnp.zeros((N_A_LAYERS, x_prompt.shape[0], CONV_WIDTH - 1, D_MODEL), x_prompt.dtype)
    y_prompt, state_conv_prompt, k_prompt, v_prompt = _trunk(x_prompt, zero_conv, None, None, weights)
    y_sample, state_conv_sample, k_sample, v_sample = _trunk(x_sample, state_conv, cache_k, cache_v, weights)
    return (y_prompt, y_sample, state_conv_prompt, k_prompt, v_prompt,
            state_conv_sample, k_sample, v_sample)
```

```python
import contextlib
import numpy as np
import concourse.bass as bass
import concourse.mybir as mybir
from concourse.bass_utils import run_bass_kernel_spmd

F32 = mybir.dt.float32
BF16 = mybir.dt.bfloat16
AF = mybir.ActivationFunctionType
ALU = mybir.AluOpType

D = 2048
KC = 16
DFF = 5632
NF = 44
NQ = 4
FQ = 11
TP = 1024
TS = 128
T = TP + TS
HALO = 32
CW = 31
EPS = 1e-6
SEM_LIMIT = 12000
NEG = -200.0


class Buf:
    __slots__ = ("name", "w", "r")

    def __init__(self, name):
        self.name = name
        self.w = None
        self.r = []


class Op:
    __slots__ = ("eng", "fn", "deps", "dma", "needs_sig", "sem", "val", "small")

    def __init__(self, eng, fn, dma, small):
        self.eng = eng
        self.fn = fn
        self.deps = []
        self.dma = dma
        self.needs_sig = False
        self.sem = None
        self.val = 0
        self.small = small


ENGS = ("pe", "act", "dve", "pool", "sp")


class Prog:
    def __init__(self, nc):
        self.nc = nc
        self.ops = {e: [] for e in ENGS}
        self.stack = contextlib.ExitStack()
        self.nsem = 0
        self.final = []
        self.last_dma = {}

    def new_sem(self, tag):
        self.nsem += 1
        return self.stack.enter_context(self.nc.semaphore(f"s{self.nsem}_{tag}"))

    def op(self, eng, fn, reads=(), writes=(), dma=None, small=False):
        o = Op(eng, fn, dma, small)
        deps = []
        for b in reads:
            if b.w is not None:
                deps.append(b.w)
        for b in writes:
            if b.w is not None:
                deps.append(b.w)
            deps.extend(b.r)
        seen = set()
        for d in deps:
            if id(d) in seen or d is o:
                continue
            seen.add(id(d))
            if d.dma is None and d.eng == eng and dma is None:
                if not d.small:
                    continue
                if eng == "pe":
                    continue
            d.needs_sig = True
            o.deps.append(d)
        for b in reads:
            b.r.append(o)
        for b in writes:
            b.w = o
            b.r = []
        self.ops[eng].append(o)
        if dma is not None:
            self.last_dma[dma] = o
        return o

    def barrier(self):
        lasts = []
        for e in ENGS:
            for o in reversed(self.ops[e]):
                if o.dma is None and o.fn is not None:
                    lasts.append(o)
                    break
        dmas = list(self.last_dma.values())
        for e in ENGS:
            o = Op(e, None, None, False)
            for d in lasts:
                if d.eng != e:
                    d.needs_sig = True
                    o.deps.append(d)
            o.deps.extend(dmas)
            self.ops[e].append(o)

    def finish(self, out_ops):
        self.final = list(out_ops)
        for o in self.final:
            o.needs_sig = True

    def emit(self):
        nc = self.nc
        dma_state = {}
        for e in ENGS:
            sem = None
            cnt = 0
            for o in self.ops[e]:
                if o.dma is not None:
                    st = dma_state.get(o.dma)
                    if st is None or st[1] + 16 > SEM_LIMIT:
                        st = [self.new_sem("d" + o.dma), 0]
                        dma_state[o.dma] = st
                    st[1] += 16
                    o.sem, o.val = st[0], st[1]
                elif o.needs_sig:
                    if sem is None or cnt + 1 > SEM_LIMIT:
                        sem = self.new_sem(e)
                        cnt = 0
                    cnt += 1
                    o.sem, o.val = sem, cnt
        finals = self.final

        def replay(ename, eng):
            waited = {}
            for o in self.ops[ename]:
                for d in o.deps:
                    k = id(d.sem)
                    if waited.get(k, 0) >= d.val:
                        continue
                    eng.wait_ge(d.sem, d.val)
                    waited[k] = d.val
                if o.fn is None:
                    continue
                ins = o.fn(eng)
                if o.dma is not None:
                    ins.then_inc(o.sem, 16)
                elif o.needs_sig:
                    ins.then_inc(o.sem, 1)
            if ename == "sp":
                for d in finals:
                    k = id(d.sem)
                    if waited.get(k, 0) >= d.val:
                        continue
                    eng.wait_ge(d.sem, d.val)
                    waited[k] = d.val

        with nc.Block() as block:
            @block.tensor
            def _(e):
                replay("pe", e)

            @block.scalar
            def _(e):
                replay("act", e)

            @block.vector
            def _(e):
                replay("dve", e)

            @block.gpsimd
            def _(e):
                replay("pool", e)

            @block.sync
            def _(e):
                replay("sp", e)
        self.stack.close()


class Arena:
    def __init__(self, nc, words):
        self.t = nc.alloc_sbuf_tensor("arena", [128, words], F32)
        self.ap = self.t.ap()
        self.words = words
        self.top = 0

    def f32(self, n):
        a = self.top
        self.top += n
        assert self.top <= self.words, (self.top, self.words)
        return self.ap[:, a:a + n]

    def bf16(self, n):
        w = (n + 1) // 2
        a = self.top
        self.top += w
        assert self.top <= self.words, (self.top, self.words)
        return self.ap[:, a:a + w].bitcast(BF16)[:, 0:n]

    def mark(self):
        return self.top

    def reset(self, m):
        self.top = m


class Ring:
    def __init__(self, aps, name):
        self.slots = [(a, Buf(f"{name}{i}")) for i, a in enumerate(aps)]
        self.i = 0

    def next(self):
        k = self.i % len(self.slots)
        self.i += 1
        return self.slots[k]


class WStream:
    def __init__(self, P, name, slots, srcs):
        self.P = P
        self.name = name
        self.slots = slots
        self.srcs = srcs
        self.issued = 0
        self.k = 0
        for _ in range(len(slots)):
            self._issue()

    def _issue(self):
        if self.issued >= len(self.srcs):
            return
        k = self.issued
        s = k % len(self.slots)
        DMA(self.P, "pool", self.slots[s][0], self.srcs[k], [], [self.slots[s][1]], f"{self.name}{s}")
        self.issued += 1

    def get(self):
        return self.slots[self.k % len(self.slots)]

    def done(self):
        self.k += 1
        self._issue()


def DMA(P, eng, out, in_, reads, writes, name):
    return P.op(eng, lambda e: e.dma_start(out=out, in_=in_), reads, writes, dma=name)


def ACT(P, out, in_, func, reads, writes, bias=None, scale=None, small=False):
    kw = {}
    if bias is not None:
        kw["bias"] = bias
    if scale is not None:
        kw["scale"] = scale
    return P.op("act", lambda e: e.activation(out, in_, func, **kw), reads, writes, small=small)


def TT(P, eng, out, in0, in1, op, reads, writes, small=False):
    return P.op(eng, lambda e: e.tensor_tensor(out, in0, in1, op), reads, writes, small=small)


def STT(P, eng, out, in0, scalar, in1, op0, op1, reads, writes, small=False):
    return P.op(eng, lambda e: e.scalar_tensor_tensor(out, in0, scalar, in1, op0, op1), reads, writes, small=small)


def TS(P, eng, out, in0, s1, s2, op0, op1, reads, writes, small=False):
    if s2 is None:
        return P.op(eng, lambda e: e.tensor_scalar(out, in0, s1, None, op0), reads, writes, small=small)
    return P.op(eng, lambda e: e.tensor_scalar(out, in0, s1, s2, op0, op1), reads, writes, small=small)


def CP(P, eng, out, in_, reads, writes, small=False):
    return P.op(eng, lambda e: e.tensor_copy(out, in_), reads, writes, small=small)


def RECIP(P, out, in_, reads, writes, small=False):
    return P.op("dve", lambda e: e.reciprocal(out, in_), reads, writes, small=small)


def MM1(P, out, lhsT, rhs, start, stop, reads, writes):
    return P.op("pe", lambda e: e.matmul(out, lhsT, rhs, start=start, stop=stop), reads, writes)


def mm_group(P, psum_ap, pbuf, lhs_list, rhs_list, reads):
    n = len(lhs_list)
    for k in range(n):
        first, last = (k == 0), (k == n - 1)
        MM1(P, psum_ap, lhs_list[k], rhs_list[k], first, last,
            reads if (first or last) else (), [pbuf] if (first or last) else ())


class Ctx:
    pass


def wtiles(ws):
    return [ws[:, k * 128:(k + 1) * 128] for k in range(ws.shape[1] // 128)]


def rmsnorm_stats(P, C, src, tiles):
    for (t0, n) in tiles:
        for kc in range(KC):
            sq, sqb = C.sq_ring.next()
            ACT(P, sq[:, 0:n], src[:, kc, t0:t0 + n], AF.Square, [C.xbuf], [sqb])
            if kc == 0:
                CP(P, "pool", C.acc[:, 0:n], sq[:, 0:n], [sqb], [C.accb])
            else:
                TT(P, "pool", C.acc[:, 0:n], C.acc[:, 0:n], sq[:, 0:n], ALU.add, [sqb, C.accb], [C.accb])
        ps, psb = C.ps_small.next()
        MM1(P, ps[:, 0:n], C.ones_mean, C.acc[:, 0:n], True, True, [C.accb, C.constb], [psb])
        ACT(P, C.rstd[:, t0:t0 + n], ps[:, 0:n], AF.Sqrt, [psb, C.constb], [C.rstdb], bias=C.eps_col, small=(n < 128))
        RECIP(P, C.rstd[:, t0:t0 + n], C.rstd[:, t0:t0 + n], [C.rstdb], [C.rstdb], small=(n < 128))


def rmsnorm_apply(P, C, src, dst, dstb, tiles, gcol, doffs=None):
    for i, (t0, n) in enumerate(tiles):
        d0 = t0 if doffs is None else doffs[i]
        for kc in range(KC):
            eng = "dve"
            STT(P, eng, dst[:, kc, d0:d0 + n], src[:, kc, t0:t0 + n], C.vecs[:, gcol + kc:gcol + kc + 1],
                C.rstd[:, t0:t0 + n], ALU.mult, ALU.mult, [C.xbuf, C.rstdb, C.constb], [dstb])


def ffn(P, C, tiles, wg, wu, wd, gcol):
    rmsnorm_stats(P, C, C.xT, tiles)
    rmsnorm_apply(P, C, C.xT, C.xn, C.xnb, tiles, gcol)
    for q in range(NQ):
        for fi in range(FQ):
            wgs, wgb = wg.get()
            wus, wub = wu.get()
            for (t0, n) in tiles:
                pg, pgb = C.ps_g.next()
                pu, pub = C.ps_u.next()
                rhs = [C.xn[:, kc, t0:t0 + n] for kc in range(KC)]
                mm_group(P, pg[:, 0:n], pgb, wtiles(wgs), rhs, [wgb, C.xnb])
                mm_group(P, pu[:, 0:n], pub, wtiles(wus), rhs, [wub, C.xnb])
                sg, sgb = C.sg_ring.next()
                ACT(P, sg[:, 0:n], pg[:, 0:n], AF.Silu, [pgb], [sgb])
                TT(P, "dve", C.hT[:, fi, t0:t0 + n], sg[:, 0:n], pu[:, 0:n], ALU.mult, [sgb, pub], [C.hTb[fi]])
            wg.done()
            wu.done()
        for m in range(KC):
            wds, wdb = wd.get()
            for (t0, n) in tiles:
                py, pyb = C.ps_y.next()
                mm_group(P, py[:, 0:n], pyb, wtiles(wds), [C.hT[:, fi, t0:t0 + n] for fi in range(FQ)], [wdb] + C.hTb)
                STT(P, "dve", C.xT[:, m, t0:t0 + n], py[:, 0:n], 0.5, C.xT[:, m, t0:t0 + n], ALU.mult, ALU.add,
                    [pyb, C.xbuf], [C.xbuf])
            wd.done()


def alloc_common(P, C, A, ntok, nvec):
    C.xT = A.f32(KC * ntok).rearrange("p (c t) -> p c t", c=KC)
    C.xbuf = Buf("xT")
    C.vecs = A.f32(nvec)
    C.ones_mean = A.f32(128)
    C.eps_col = A.f32(1)
    C.constb = Buf("const")
    C.rstd = A.f32(ntok)
    C.rstdb = Buf("rstd")
    C.acc = A.f32(512)
    C.accb = Buf("acc")
    C.sq_ring = Ring([A.f32(512) for _ in range(2)], "sq")
    C.sg_ring = Ring([A.f32(512) for _ in range(2)], "sg")


def alloc_ffn(C, A, ntok):
    C.xn = A.bf16(KC * ntok).rearrange("p (c t) -> p c t", c=KC)
    C.xnb = Buf("xn")
    C.hT = A.bf16(FQ * ntok).rearrange("p (c t) -> p c t", c=FQ)
    C.hTb = [Buf(f"hT{i}") for i in range(FQ)]
    C.wg_slots = [(A.bf16(KC * 128), Buf(f"wg{i}")) for i in range(2)]
    C.wu_slots = [(A.bf16(KC * 128), Buf(f"wu{i}")) for i in range(2)]
    C.wd_slots = [(A.bf16(FQ * 128), Buf(f"wd{i}")) for i in range(2)]


def psum_alloc(nc, P, n):
    banks = []
    for i in range(n):
        t = P.stack.enter_context(nc.psum_tensor(f"ps{i}", [128, 512], F32))
        banks.append(t[:])
    return banks


VA = {}
_o = 0
for _nm, _w in [("ffn1", 16), ("mix", 16), ("ffn2", 16), ("kvn", 16), ("b1a", 16), ("b1b", 16), ("dwb", 16),
                ("lng", 16), ("lnb", 16), ("b2", 16), ("dww", 16 * CW), ("flag", 1)]:
    VA[_nm] = _o
    _o += _w
NVA = _o


def build_A(stop_after=None):
    nc = bass.Bass("TRN2", target_bir_lowering=False)
    NT = T + HALO
    d = lambda name, shape, kind: nc.dram_tensor(name, shape, F32, kind=kind).ap()
    xT_in = d("xT_in", [D, NT], "ExternalInput")
    stT_in = d("stT_in", [D, 2 * 30], "ExternalInput")
    vecs_in = d("vecs", [128, NVA], "ExternalInput")
    cst_in = d("cst", [128, 128], "ExternalInput")
    eps_in = d("epsc", [128, 1], "ExternalInput")
    g1 = d("g1", [NF, 128, KC * 128], "ExternalInput")
    u1 = d("u1", [NF, 128, KC * 128], "ExternalInput")
    d1 = d("d1", [NQ * KC, 128, FQ * 128], "ExternalInput")
    g2 = d("g2", [NF, 128, KC * 128], "ExternalInput")
    u2 = d("u2", [NF, 128, KC * 128], "ExternalInput")
    d2 = d("d2", [NQ * KC, 128, FQ * 128], "ExternalInput")
    w1 = d("w1", [2 * KC, 128, KC * 128], "ExternalInput")
    w2 = d("w2", [KC, 128, KC * 128], "ExternalInput")
    wkv = d("wkv", [2 * KC, 128, KC * 128], "ExternalInput")
    xT_out = d("xT_out", [D, T], "ExternalOutput")
    kvT_out = d("kvT_out", [2 * D, T], "ExternalOutput")
    stp_out = d("stp_out", [D, 30], "ExternalOutput")
    sts_out = d("sts_out", [D, 2 * 30], "ExternalOutput")

    P = Prog(nc)
    C = Ctx()
    A = Arena(nc, 53000)
    alloc_common(P, C, A, NT, NVA)
    banks = psum_alloc(nc, P, 8)
    C.ps_g = Ring(banks[0:2], "psg")
    C.ps_u = Ring(banks[2:4], "psu")
    C.ps_y = Ring(banks[4:6], "psy")
    C.ps_small = Ring(banks[6:8], "pss")

    DMA(P, "sp", C.vecs, vecs_in, [], [C.constb], "c0")
    DMA(P, "sp", C.ones_mean, cst_in, [], [C.constb], "c0")
    DMA(P, "sp", C.eps_col, eps_in, [], [C.constb], "c0")
    xin = xT_in.rearrange("(c p) t -> p c t", p=128)
    for kc in range(KC):
        DMA(P, "sp", C.xT[:, kc, :], xin[:, kc, :], [], [C.xbuf], "cx")

    mark = A.mark()
    alloc_ffn(C, A, NT)
    tiles_h = [(0, 512), (512, 512), (1024, 128), (1152, 32)]
    tiles = [(0, 512), (512, 512), (1024, 128)]

    def wstreams(g, u, dd):
        return (WStream(P, "wg", C.wg_slots, [g[f] for f in range(NF)]),
                WStream(P, "wu", C.wu_slots, [u[f] for f in range(NF)]),
                WStream(P, "wd", C.wd_slots, [dd[i] for i in range(NQ * KC)]))

    finals = []
    wg, wu, wd = wstreams(g1, u1, d1)
    ffn(P, C, tiles_h, wg, wu, wd, VA["ffn1"])

    P.barrier()
    A.reset(mark)
    hn = A.bf16(KC * 544).rearrange("p (c t) -> p c t", c=KC)
    hnb = Buf("hn")
    cT = A.f32(KC * 512).rearrange("p (c t) -> p c t", c=KC)
    cTb = [Buf(f"cT{i}") for i in range(KC)]
    sT = A.bf16(KC * 512).rearrange("p (c t) -> p c t", c=KC)
    sTb = Buf("sT")
    uext_ring = Ring([A.f32(30 + 512) for _ in range(2)], "uext")
    pref = A.f32(KC * 30).rearrange("p (c k) -> p c k", c=KC)
    prefb = [Buf(f"pref{i}") for i in range(KC)]
    stS = A.f32(KC * 60).rearrange("p (c k) -> p c k", c=KC)
    stSb = Buf("stS")
    stO = A.f32(KC * 60).rearrange("p (c k) -> p c k", c=KC)
    stOb = Buf("stO")
    s1, s1b = A.f32(512), Buf("s1")
    s2, s2b = A.f32(512), Buf("s2")
    mean, meanb = A.f32(512), Buf("mean")
    rs, rsb = A.f32(512), Buf("rs")
    w1a_slots = [(A.bf16(KC * 128), Buf(f"w1a{i}")) for i in range(2)]
    w1b_slots = [(A.bf16(KC * 128), Buf(f"w1b{i}")) for i in range(2)]
    w2_slots = [(A.bf16(KC * 128), Buf(f"w2{i}")) for i in range(2)]
    DMA(P, "sp", stS, stT_in.rearrange("(c p) k -> p c k", p=128), [], [stSb], "c1")

    conv_tiles = [("p0", 0, 512), ("p1", 512, 512), ("s", 1024, 128)]
    w1a = WStream(P, "w1a", w1a_slots, [w1[m] for _ in conv_tiles for m in range(KC)])
    w1b = WStream(P, "w1b", w1b_slots, [w1[KC + m] for _ in conv_tiles for m in range(KC)])
    w2s = WStream(P, "w2", w2_slots, [w2[m] for _ in conv_tiles for m in range(KC)])
    vv = C.vecs

    def vc(name, m):
        return vv[:, VA[name] + m:VA[name] + m + 1]

    for (kind, t0, n) in conv_tiles:
        nt = [(t0, n)] + ([(T, 32)] if kind == "p0" else [])
        rmsnorm_stats(P, C, C.xT, nt)
        rmsnorm_apply(P, C, C.xT, hn, hnb, nt, VA["mix"], doffs=[0, 512])
        segs = [(30, 0, n)] if kind != "s" else [(30, 0, 64), (124, 64, 64)]
        for m in range(KC):
            was, wab = w1a.get()
            wbs, wbb = w1b.get()
            ue, ueb = uext_ring.next()
            if kind == "p0":
                pa, pab = C.ps_small.next()
                pb, pbb = C.ps_small.next()
                rh = [hn[:, kc, 512:544] for kc in range(KC)]
                mm_group(P, pa[:, 0:32], pab, wtiles(was), rh, [wab, hnb])
                mm_group(P, pb[:, 0:32], pbb, wtiles(wbs), rh, [wbb, hnb])
                sg, sgb = C.sg_ring.next()
                ACT(P, sg[:, 0:32], pb[:, 0:32], AF.Sigmoid, [pbb, C.constb], [sgb], bias=vc("b1b", m), small=True)
                STT(P, "dve", sg[:, 0:32], pa[:, 0:32], vc("b1a", m), sg[:, 0:32], ALU.add, ALU.mult,
                    [pab, sgb, C.constb], [sgb], small=True)
                TS(P, "dve", ue[:, 0:30], sg[:, 2:32], vc("flag", 0), None, ALU.mult, None,
                   [sgb, C.constb], [ueb], small=True)
            elif kind == "p1":
                CP(P, "pool", ue[:, 0:30], pref[:, m, :], [prefb[m]], [ueb], small=True)
            else:
                CP(P, "pool", ue[:, 0:30], stS[:, m, 0:30], [stSb], [ueb], small=True)
                CP(P, "pool", ue[:, 94:124], stS[:, m, 30:60], [stSb], [ueb], small=True)
            pa, pab = C.ps_g.next()
            pb, pbb = C.ps_u.next()
            rh = [hn[:, kc, 0:n] for kc in range(KC)]
            mm_group(P, pa[:, 0:n], pab, wtiles(was), rh, [wab, hnb])
            mm_group(P, pb[:, 0:n], pbb, wtiles(wbs), rh, [wbb, hnb])
            w1a.done()
            w1b.done()
            sg, sgb = C.sg_ring.next()
            ACT(P, sg[:, 0:n], pb[:, 0:n], AF.Sigmoid, [pbb, C.constb], [sgb], bias=vc("b1b", m))
            for (eo, to, ln) in segs:
                STT(P, "dve", ue[:, eo:eo + ln], pa[:, to:to + ln], vc("b1a", m), sg[:, to:to + ln], ALU.add, ALU.mult,
                    [pab, sgb, C.constb], [ueb], small=True)
            if kind != "s":
                CP(P, "pool", pref[:, m, :], ue[:, n:n + 30], [ueb], [prefb[m]], small=True)
            else:
                CP(P, "pool", stO[:, m, 0:30], ue[:, 64:94], [ueb], [stOb], small=True)
                CP(P, "pool", stO[:, m, 30:60], ue[:, 158:188], [ueb], [stOb], small=True)
            for (eo, to, ln) in segs:
                b0 = eo - 30
                for k in range(CW):
                    wk = vv[:, VA["dww"] + m * CW + k:VA["dww"] + m * CW + k + 1]
                    if k == 0:
                        TS(P, "dve", cT[:, m, to:to + ln], ue[:, b0:b0 + ln], wk, vc("dwb", m), ALU.mult, ALU.add,
                           [ueb, C.constb], [cTb[m]], small=(ln < 128))
                    else:
                        STT(P, "dve", cT[:, m, to:to + ln], ue[:, b0 + k:b0 + k + ln], wk, cT[:, m, to:to + ln],
                            ALU.mult, ALU.add, [ueb, cTb[m], C.constb], [cTb[m]], small=(ln < 128))
            sq, sqb = C.sq_ring.next()
            ACT(P, sq[:, 0:n], cT[:, m, 0:n], AF.Square, [cTb[m]], [sqb])
            if m == 0:
                CP(P, "pool", s1[:, 0:n], cT[:, m, 0:n], [cTb[m]], [s1b])
                CP(P, "pool", s2[:, 0:n], sq[:, 0:n], [sqb], [s2b])
            else:
                TT(P, "pool", s1[:, 0:n], s1[:, 0:n], cT[:, m, 0:n], ALU.add, [cTb[m], s1b], [s1b])
                TT(P, "pool", s2[:, 0:n], s2[:, 0:n], sq[:, 0:n], ALU.add, [sqb, s2b], [s2b])
        p1, p1b = C.ps_small.next()
        p2, p2b = C.ps_small.next()
        MM1(P, p1[:, 0:n], C.ones_mean, s1[:, 0:n], True, True, [s1b, C.constb], [p1b])
        MM1(P, p2[:, 0:n], C.ones_mean, s2[:, 0:n], True, True, [s2b, C.constb], [p2b])
        ACT(P, mean[:, 0:n], p1[:, 0:n], AF.Identity, [p1b], [meanb])
        TT(P, "dve", rs[:, 0:n], mean[:, 0:n], mean[:, 0:n], ALU.mult, [meanb], [rsb])
        TT(P, "dve", rs[:, 0:n], p2[:, 0:n], rs[:, 0:n], ALU.subtract, [p2b, rsb], [rsb])
        ACT(P, rs[:, 0:n], rs[:, 0:n], AF.Sqrt, [rsb, C.constb], [rsb], bias=C.eps_col)
        RECIP(P, rs[:, 0:n], rs[:, 0:n], [rsb], [rsb])
        for m in range(KC):
            t1, t1b = C.sq_ring.next()
            t2, t2b = C.sg_ring.next()
            TT(P, "dve", t1[:, 0:n], cT[:, m, 0:n], mean[:, 0:n], ALU.subtract, [cTb[m], meanb], [t1b])
            TT(P, "pool", t2[:, 0:n], t1[:, 0:n], rs[:, 0:n], ALU.mult, [t1b, rsb], [t2b])
            ACT(P, sT[:, m, 0:n], t2[:, 0:n], AF.Silu, [t2b, C.constb], [sTb], bias=vc("lnb", m), scale=vc("lng", m))
        for m2 in range(KC):
            w2t, w2b = w2s.get()
            py, pyb = C.ps_y.next()
            mm_group(P, py[:, 0:n], pyb, wtiles(w2t), [sT[:, kc, 0:n] for kc in range(KC)], [w2b, sTb])
            w2s.done()
            STT(P, "dve", C.xT[:, m2, t0:t0 + n], py[:, 0:n], vc("b2", m2), C.xT[:, m2, t0:t0 + n], ALU.add, ALU.add,
                [pyb, C.xbuf, C.constb], [C.xbuf])
        if kind == "p1":
            finals.append(DMA(P, "sp", stp_out.rearrange("(c p) k -> p c k", p=128), pref, prefb, [], "o0"))
        if kind == "s":
            finals.append(DMA(P, "sp", sts_out.rearrange("(c p) k -> p c k", p=128), stO, [stOb], [], "o1"))

    P.barrier()
    A.reset(mark)
    alloc_ffn(C, A, NT)
    wg, wu, wd = wstreams(g2, u2, d2)
    ffn(P, C, tiles, wg, wu, wd, VA["ffn2"])
    finals.append(DMA(P, "sp", xT_out.rearrange("(c p) t -> p c t", p=128), C.xT[:, :, 0:T], [C.xbuf], [], "o2"))

    rmsnorm_stats(P, C, C.xT, tiles)
    rmsnorm_apply(P, C, C.xT, C.xn, C.xnb, tiles, VA["kvn"])
    wk = WStream(P, "wkv", C.wg_slots, [wkv[i] for i in range(2 * KC)])
    stg = Ring([A.f32(512) for _ in range(3)], "stg")
    kvo = kvT_out.rearrange("(c p) t -> p c t", p=128)
    last = {}
    for nb in range(2 * KC):
        ws, wb = wk.get()
        for (t0, n) in tiles:
            pk, pkb = C.ps_g.next()
            mm_group(P, pk[:, 0:n], pkb, wtiles(ws), [C.xn[:, kc, t0:t0 + n] for kc in range(KC)], [wb, C.xnb])
            slot = stg.i % 3
            st, stb = stg.next()
            ACT(P, st[:, 0:n], pk[:, 0:n], AF.Identity, [pkb], [stb])
            last[slot] = DMA(P, "sp", kvo[:, nb, t0:t0 + n], st[:, 0:n], [stb], [], f"kv{slot}")
        wk.done()
    P.finish(finals + list(last.values()))
    P.emit()
    return nc


VB = {}
_o = 0
for _nm, _w in [("ffn1", 16), ("mix", 16), ("ffn2", 16), ("fin", 16), ("sbias", 3)]:
    VB[_nm] = _o
    _o += _w
NVB = _o
NH = 16
KSLOT = 4 * 1088
VSLOT = 4 * 9 * 128


class RingS:
    def __init__(self, slots):
        self.slots = slots
        self.i = 0

    def next(self):
        k = self.i % len(self.slots)
        self.i += 1
        return self.slots[k]


def attention(P, C, A, banks, kslots, vslots, kp_in, vp_in, ks_in, vs_in):
    zr = RingS(banks[0:4])
    rr = RingS(banks[4:6])
    Ob = banks[6:8]
    negRs = [(A.f32(512), Buf(f"negR{i}")) for i in range(2)]
    e_ring = Ring([A.f32(512) for _ in range(2)], "e")
    sp_ring = Ring([A.f32(512) for _ in range(3)], "sp")
    hi_ring = Ring([A.bf16(512) for _ in range(3)], "hi")
    lo_ring = Ring([A.bf16(512) for _ in range(3)], "lo")
    arg_ring = Ring([A.f32(512) for _ in range(2)], "arg")
    a_ring = Ring([A.bf16(512) for _ in range(3)], "aT")
    vv = C.vecs

    groups = []
    for h in range(NH):
        sl = h % 2
        kT, kb_ = kslots[sl]
        vS, vb_ = vslots[sl]
        loads = [(kT[:, 0:4096], kp_in[h], kb_, f"kld{sl}"), (vS[:, 0:4096], vp_in[h], vb_, f"vld{sl}")]
        streams = []
        for qt in range(2):
            q_ap = C.QT[:, h, qt * 512:(qt + 1) * 512]
            blocks = []
            for kb in range(4 * qt + 3, -1, -1):
                r = kb - 4 * qt
                blocks.append(dict(K=[(kT[:, kb * 128:(kb + 1) * 128], q_ap, 0, 512)], kk=128, nc=512, bias=None,
                                   mask=(C.masks[:, r, :] if r >= 0 else None),
                                   V=[(vS[:, kb * 128:(kb + 1) * 128], 0, 512)], kvb=[kb_, vb_]))
            for slot in range(3):
                for kb in range(7, -1, -1):
                    g = 8 * (slot + 1) + kb
                    blocks.append(dict(K=[(kT[:, g * 128:(g + 1) * 128], q_ap, 0, 512)], kk=128, nc=512,
                                       bias=vv[:, VB["sbias"] + slot:VB["sbias"] + slot + 1], mask=None,
                                       V=[(vS[:, g * 128:(g + 1) * 128], 0, 512)], kvb=[kb_, vb_]))
            streams.append(dict(blocks=blocks, finals=[(C.QT[:, h, qt * 512:(qt + 1) * 512], 0, 512)],
                                qb=[C.QTb[h][qt]], negR=negRs[qt], O=Ob[qt]))
        groups.append((loads, streams))
    for pair in range(4):
        loads, streams = [], []
        for u in range(2):
            unit = pair * 2 + u
            seq, g4 = unit // 4, unit % 4
            kT, kb_ = kslots[u]
            vS, vb_ = vslots[u]
            loads += [(kT, ks_in[unit], kb_, f"kld{u}"), (vS, vs_in[unit], vb_, f"vld{u}")]
            k3 = kT.rearrange("p (h k) -> p h k", h=4)
            v4 = vS.rearrange("p (h b d) -> p h b d", h=4, b=9)
            tq = TP + 64 * seq
            blocks = []
            for kb in [8] + list(range(7, -1, -1)):
                kk = 64 if kb == 8 else 128
                blocks.append(dict(
                    K=[(k3[:, hh, kb * 128:kb * 128 + kk], C.QT[:, 4 * g4 + hh, tq:tq + 64], hh * 64, 64) for hh in range(4)],
                    kk=kk, nc=256, bias=None, mask=(C.smask if kb == 8 else None),
                    V=[(v4[0:kk, hh, kb, :], hh * 64, 64) for hh in range(4)], kvb=[kb_, vb_]))
            streams.append(dict(blocks=blocks,
                                finals=[(C.QT[:, 4 * g4 + hh, tq:tq + 64], hh * 64, 64) for hh in range(4)],
                                qb=[C.QTb[4 * g4 + hh][2] for hh in range(4)], negR=negRs[u], O=Ob[u]))
        groups.append((loads, streams))

    seq_blocks = []
    for gi, (loads, streams) in enumerate(groups):
        n = max(len(s["blocks"]) for s in streams)
        first = True
        for i in range(n):
            for s in streams:
                if i < len(s["blocks"]):
                    b = s["blocks"][i]
                    b["stream"] = s
                    b["first"] = (i == 0)
                    b["last"] = (i == len(s["blocks"]) - 1)
                    b["loads"] = loads if first else None
                    first = False
                    seq_blocks.append(b)

    def stage01(b):
        if b["loads"] is not None:
            for (dst, src, buf, nm) in b["loads"]:
                DMA(P, "pool", dst, src, [], [buf], nm)
        s = b["stream"]
        kk, ncl = b["kk"], b["nc"]
        pz, pzb = zr.next()
        b["pz"], b["pzb"] = pz, pzb
        nK = len(b["K"])
        for i, (kap, qap, c0, cn) in enumerate(b["K"]):
            MM1(P, pz[0:kk, c0:c0 + cn], kap, qap, (i == 0), False, b["kvb"] + s["qb"], [pzb])
        e, eb = e_ring.next()
        sp, spb = sp_ring.next()
        bias = b["bias"]
        ACT(P, e[0:kk, 0:ncl], pz[0:kk, 0:ncl], AF.Exp, [pzb, C.constb], [eb], bias=bias)
        ACT(P, sp[0:kk, 0:ncl], e[0:kk, 0:ncl], AF.Ln, [eb], [spb], bias=1.0)
        if b["mask"] is not None:
            TT(P, "dve", sp[0:kk, 0:ncl], sp[0:kk, 0:ncl], b["mask"][0:kk, 0:ncl], ALU.mult, [spb, C.constb], [spb])
        hi, hib = hi_ring.next()
        lo, lob = lo_ring.next()
        CP(P, "dve", hi[0:kk, 0:ncl], sp[0:kk, 0:ncl], [spb], [hib])
        TT(P, "pool", lo[0:kk, 0:ncl], sp[0:kk, 0:ncl], hi[0:kk, 0:ncl], ALU.subtract, [spb, hib], [lob])
        b["hi"], b["hib"], b["lo"], b["lob"] = hi, hib, lo, lob

    def stage23(b):
        s = b["stream"]
        kk, ncl = b["kk"], b["nc"]
        pz, pzb = b["pz"], b["pzb"]
        negR, negRb = s["negR"]
        hi, hib, lo, lob = b["hi"], b["hib"], b["lo"], b["lob"]
        MM1(P, pz[0:kk, 0:ncl], C.ut[0:kk, 0:kk], hi[0:kk, 0:ncl], False, False, [hib, C.castb], [pzb])
        MM1(P, pz[0:kk, 0:ncl], C.ut[0:kk, 0:kk], lo[0:kk, 0:ncl], False, True, [lob], [pzb])
        aT, aTb = a_ring.next()
        if b["first"]:
            ACT(P, aT[0:kk, 0:ncl], pz[0:kk, 0:ncl], AF.Exp, [pzb, C.constb], [aTb], bias=b["bias"])
        else:
            arg, argb = arg_ring.next()
            TT(P, "dve", arg[0:kk, 0:ncl], pz[0:kk, 0:ncl], negR[0:kk, 0:ncl], ALU.add, [pzb, negRb], [argb])
            ACT(P, aT[0:kk, 0:ncl], arg[0:kk, 0:ncl], AF.Exp, [argb, C.constb], [aTb], bias=b["bias"])
        if b["mask"] is not None:
            TT(P, "pool", aT[0:kk, 0:ncl], aT[0:kk, 0:ncl], b["mask"][0:kk, 0:ncl], ALU.mult, [aTb, C.constb], [aTb])
        b["aT"], b["aTb"] = aT, aTb
        if not b["last"]:
            pR, pRb = rr.next()
            MM1(P, pR[:, 0:ncl], C.ones_bf[0:kk, :], hi[0:kk, 0:ncl], True, False, [hib, C.castb], [pRb])
            MM1(P, pR[:, 0:ncl], C.ones_bf[0:kk, :], lo[0:kk, 0:ncl], False, True, [lob], [pRb])
            if b["first"]:
                TS(P, "dve", negR[:, 0:ncl], pR[:, 0:ncl], -1.0, None, ALU.mult, None, [pRb], [negRb])
            else:
                TT(P, "dve", negR[:, 0:ncl], negR[:, 0:ncl], pR[:, 0:ncl], ALU.subtract, [negRb, pRb], [negRb])

    def stage4(b):
        s = b["stream"]
        kk = b["kk"]
        pO, pOb = s["O"]
        aT, aTb = b["aT"], b["aTb"]
        for vi, (vap, c0, cn) in enumerate(b["V"]):
            MM1(P, pO[:, c0:c0 + cn], vap, aT[0:kk, c0:c0 + cn], (b["first"] and vi == 0), b["last"], [aTb] + b["kvb"], [pOb])
        if b["last"]:
            for (dst, c0, cn) in s["finals"]:
                ACT(P, dst, pO[:, c0:c0 + cn], AF.Identity, [pOb], s["qb"])

    nb = len(seq_blocks)
    for t in range(nb + 4):
        if t < nb:
            stage01(seq_blocks[t])
        if 0 <= t - 2 < nb:
            stage23(seq_blocks[t - 2])
        if 0 <= t - 4 < nb:
            stage4(seq_blocks[t - 4])


def build_B():
    nc = bass.Bass("TRN2", target_bir_lowering=False)
    d = lambda name, shape, kind: nc.dram_tensor(name, shape, F32, kind=kind).ap()
    xT_in = d("xT_in", [D, T], "ExternalInput")
    vecs_in = d("vecs", [128, NVB], "ExternalInput")
    cst_in = d("cst", [128, 128], "ExternalInput")
    eps_in = d("epsc", [128, 1], "ExternalInput")
    tsl_in = d("tsl", [128, 256], "ExternalInput")
    masks_in = d("masks", [128, 4 * 512], "ExternalInput")
    smask_in = d("smask", [128, 256], "ExternalInput")
    g1 = d("g1", [NF, 128, KC * 128], "ExternalInput")
    u1 = d("u1", [NF, 128, KC * 128], "ExternalInput")
    d1 = d("d1", [NQ * KC, 128, FQ * 128], "ExternalInput")
    g2 = d("g2", [NF, 128, KC * 128], "ExternalInput")
    u2 = d("u2", [NF, 128, KC * 128], "ExternalInput")
    d2 = d("d2", [NQ * KC, 128, FQ * 128], "ExternalInput")
    wq = d("wq", [KC, 128, KC * 128], "ExternalInput")
    wo = d("wo", [KC, 128, KC * 128], "ExternalInput")
    kp_in = d("kp", [NH, 128, 4096], "ExternalInput")
    vp_in = d("vp", [NH, 128, 4096], "ExternalInput")
    ks_in = d("ks", [8, 128, KSLOT], "ExternalInput")
    vs_in = d("vs", [8, 128, VSLOT], "ExternalInput")
    yT_out = d("yT_out", [D, T], "ExternalOutput")

    P = Prog(nc)
    C = Ctx()
    A = Arena(nc, 53000)
    alloc_common(P, C, A, T, NVB)
    tsl_all = A.bf16(256)
    C.ut = tsl_all[:, 0:128]
    C.ones_bf = tsl_all[:, 128:256]
    C.masks = A.f32(4 * 512).rearrange("p (r t) -> p r t", r=4)
    C.smask = A.f32(256)
    bank_aps = psum_alloc(nc, P, 8)
    banks = [(bank_aps[i], Buf(f"bank{i}")) for i in range(8)]
    C.ps_g = RingS(banks[0:2])
    C.ps_u = RingS(banks[2:4])
    C.ps_y = RingS(banks[4:6])
    C.ps_small = RingS(banks[6:8])

    DMA(P, "sp", C.vecs, vecs_in, [], [C.constb], "c0")
    DMA(P, "sp", C.ones_mean, cst_in, [], [C.constb], "c0")
    DMA(P, "sp", C.eps_col, eps_in, [], [C.constb], "c0")
    DMA(P, "sp", C.masks, masks_in.rearrange("p (r t) -> p r t", r=4), [], [C.constb], "c0")
    DMA(P, "sp", C.smask, smask_in, [], [C.constb], "c0")
    C.castb = Buf("castload")
    DMA(P, "pool", tsl_all, tsl_in, [], [C.castb], "c2")
    xin = xT_in.rearrange("(c p) t -> p c t", p=128)
    for kc in range(KC):
        DMA(P, "sp", C.xT[:, kc, :], xin[:, kc, :], [], [C.xbuf], "cx")

    mark = A.mark()
    alloc_ffn(C, A, T)
    tiles = [(0, 512), (512, 512), (1024, 128)]

    def wstreams(g, u, dd):
        return (WStream(P, "wg", C.wg_slots, [g[f] for f in range(NF)]),
                WStream(P, "wu", C.wu_slots, [u[f] for f in range(NF)]),
                WStream(P, "wd", C.wd_slots, [dd[i] for i in range(NQ * KC)]))

    wg, wu, wd = wstreams(g1, u1, d1)
    ffn(P, C, tiles, wg, wu, wd, VB["ffn1"])

    rmsnorm_stats(P, C, C.xT, tiles)
    rmsnorm_apply(P, C, C.xT, C.xn, C.xnb, tiles, VB["mix"])
    P.barrier()
    HW = (KC * T) // 2
    A.reset(mark + HW)
    C.QT = A.bf16(NH * T).rearrange("p (h t) -> p h t", h=NH)
    C.QTb = [[Buf(f"q{h}_{i}") for i in range(3)] for h in range(NH)]
    wq_slots = [(A.bf16(KC * 128), Buf(f"wq{i}")) for i in range(2)]
    after_q = A.mark()
    wqs = WStream(P, "wq", wq_slots, [wq[h] for h in range(NH)])
    scale = float(128 ** -0.5)
    for h in range(NH):
        ws, wb = wqs.get()
        for ti, (t0, n) in enumerate(tiles):
            pq, pqb = C.ps_g.next()
            mm_group(P, pq[:, 0:n], pqb, wtiles(ws), [C.xn[:, kc, t0:t0 + n] for kc in range(KC)], [wb, C.xnb])
            ACT(P, C.QT[:, h, t0:t0 + n], pq[:, 0:n], AF.Identity, [pqb], [C.QTb[h][ti]], scale=scale)
        wqs.done()

    P.barrier()
    A.reset(mark)
    kslots = [(A.bf16(KSLOT), Buf(f"ks{i}")) for i in range(2)]
    vslots = [(A.bf16(VSLOT), Buf(f"vs{i}")) for i in range(2)]
    assert A.mark() <= mark + HW
    A.reset(after_q)
    attention(P, C, A, banks, kslots, vslots, kp_in, vp_in, ks_in, vs_in)

    wo_slots = wq_slots
    wos = WStream(P, "wo", wo_slots, [wo[m] for m in range(KC)])
    for m in range(KC):
        ws, wb = wos.get()
        for ti, (t0, n) in enumerate(tiles):
            py, pyb = C.ps_y.next()
            qbs = [C.QTb[h][ti] for h in range(NH)]
            mm_group(P, py[:, 0:n], pyb, wtiles(ws), [C.QT[:, h, t0:t0 + n] for h in range(NH)], [wb] + qbs)
            TT(P, "dve", C.xT[:, m, t0:t0 + n], py[:, 0:n], C.xT[:, m, t0:t0 + n], ALU.add, [pyb, C.xbuf], [C.xbuf])
        wos.done()

    P.barrier()
    A.reset(mark)
    alloc_ffn(C, A, T)
    wg, wu, wd = wstreams(g2, u2, d2)
    ffn(P, C, tiles, wg, wu, wd, VB["ffn2"])

    rmsnorm_stats(P, C, C.xT, tiles)
    stg = Ring([A.f32(512) for _ in range(3)], "stg")
    yo = yT_out.rearrange("(c p) t -> p c t", p=128)
    last = {}
    for (t0, n) in tiles:
        for kc in range(KC):
            slot = stg.i % 3
            st, stb = stg.next()
            STT(P, "dve", st[:, 0:n], C.xT[:, kc, t0:t0 + n], C.vecs[:, VB["fin"] + kc:VB["fin"] + kc + 1],
                C.rstd[:, t0:t0 + n], ALU.mult, ALU.mult, [C.xbuf, C.rstdb, C.constb], [stb])
            last[slot] = DMA(P, "sp", yo[:, kc, t0:t0 + n], st[:, 0:n], [stb], [], f"y{slot}")
    P.finish(list(last.values()))
    P.emit()
    return nc


def tile_w(W, kc):
    K, N = W.shape
    return np.ascontiguousarray(W.reshape(kc, 128, N // 128, 128).transpose(2, 1, 0, 3)).reshape(N // 128, 128, kc * 128)


def tile_wd(W):
    return np.ascontiguousarray(W.reshape(NQ, FQ, 128, KC, 128).transpose(0, 3, 2, 1, 4)).reshape(NQ * KC, 128, FQ * 128)


def col(v):
    return np.ascontiguousarray(np.asarray(v, np.float32).reshape(KC, 128).T)


def run_A(inp):
    f = lambda k: np.asarray(inp[k], np.float32)
    xp, xs, stc = f("x_prompt"), f("x_sample"), f("state_conv")
    shared = {
        "g1": tile_w(f("ffn1_w_gate")[0], KC), "u1": tile_w(f("ffn1_w_up")[0], KC), "d1": tile_wd(f("ffn1_w_down")[0]),
        "g2": tile_w(f("ffn2_w_gate")[0], KC), "u2": tile_w(f("ffn2_w_up")[0], KC), "d2": tile_wd(f("ffn2_w_down")[0]),
        "w1": tile_w(f("conv_pw1_w")[0], KC), "w2": tile_w(f("conv_pw2_w")[0], KC), "wkv": tile_w(f("w_kv"), KC),
    }
    shared["cst"] = np.full((128, 128), 1.0 / D, np.float32)
    shared["epsc"] = np.full((128, 1), EPS, np.float32)
    vec = np.zeros((128, NVA), np.float32)
    vec[:, VA["ffn1"]:VA["ffn1"] + 16] = col(f("ffn1_norm")[0])
    vec[:, VA["mix"]:VA["mix"] + 16] = col(f("mix_norm")[0])
    vec[:, VA["ffn2"]:VA["ffn2"] + 16] = col(f("ffn2_norm")[0])
    vec[:, VA["kvn"]:VA["kvn"] + 16] = col(f("kv_norm"))
    b1 = f("conv_pw1_b")[0]
    vec[:, VA["b1a"]:VA["b1a"] + 16] = col(b1[:D])
    vec[:, VA["b1b"]:VA["b1b"] + 16] = col(b1[D:])
    vec[:, VA["dwb"]:VA["dwb"] + 16] = col(f("conv_dw_b")[0])
    vec[:, VA["lng"]:VA["lng"] + 16] = col(f("conv_ln_g")[0])
    vec[:, VA["lnb"]:VA["lnb"] + 16] = col(f("conv_ln_b")[0])
    vec[:, VA["b2"]:VA["b2"] + 16] = col(f("conv_pw2_b")[0])
    dww = f("conv_dw_w")[0]
    vec[:, VA["dww"]:VA["dww"] + 16 * CW] = dww.T.reshape(KC, 128, CW).transpose(1, 0, 2).reshape(128, KC * CW)
    in_maps = []
    for c in range(8):
        b, j = c // 4, c % 4
        xt = np.zeros((D, T + HALO), np.float32)
        xt[:, 0:TP] = xp[b, j * TP:(j + 1) * TP].T
        xt[:, TP:TP + 64] = xs[2 * c].T
        xt[:, TP + 64:T] = xs[2 * c + 1].T
        v = vec.copy()
        if j > 0:
            xt[:, T:] = xp[b, j * TP - HALO:j * TP].T
            v[:, VA["flag"]] = 1.0
        st = np.concatenate([stc[0, 2 * c].T, stc[0, 2 * c + 1].T], axis=1)
        m = dict(shared)
        m.update({"xT_in": xt, "stT_in": np.ascontiguousarray(st), "vecs": v})
        in_maps.append(m)
    nc = build_A()
    res = run_bass_kernel_spmd(nc, in_maps, core_ids=list(range(8)))
    return [{k: np.asarray(v) for k, v in r.items()} for r in res.results]


def run_B(inp, ra):
    f = lambda k: np.asarray(inp[k], np.float32)
    ck, cv = f("cache_k"), f("cache_v")
    shared = {
        "g1": tile_w(f("ffn1_w_gate")[1], KC), "u1": tile_w(f("ffn1_w_up")[1], KC), "d1": tile_wd(f("ffn1_w_down")[1]),
        "g2": tile_w(f("ffn2_w_gate")[1], KC), "u2": tile_w(f("ffn2_w_up")[1], KC), "d2": tile_wd(f("ffn2_w_down")[1]),
        "wq": tile_w(f("attn_wq")[0], KC), "wo": tile_w(f("attn_wo")[0], KC),
        "cst": np.full((128, 128), 1.0 / D, np.float32), "epsc": np.full((128, 1), EPS, np.float32),
    }
    jj, ss = np.arange(128)[:, None], np.arange(128)[None, :]
    tsl = np.concatenate([-(jj >= ss).astype(np.float32), np.ones((128, 128), np.float32)], axis=1)
    shared["tsl"] = tsl
    tq = np.arange(512)[None, :]
    masks = np.stack([((r * 128 + jj) < tq).astype(np.float32) for r in range(4)], axis=1)
    shared["masks"] = np.ascontiguousarray(masks.reshape(128, 4 * 512))
    sm = np.zeros((128, 256), np.float32)
    sm[:64] = np.tile((np.arange(64)[:, None] < np.arange(64)[None, :]).astype(np.float32), (1, 4))
    shared["smask"] = sm
    vec = np.zeros((128, NVB), np.float32)
    vec[:, VB["ffn1"]:VB["ffn1"] + 16] = col(f("ffn1_norm")[1])
    vec[:, VB["mix"]:VB["mix"] + 16] = col(f("mix_norm")[1])
    vec[:, VB["ffn2"]:VB["ffn2"] + 16] = col(f("ffn2_norm")[1])
    vec[:, VB["fin"]:VB["fin"] + 16] = col(f("final_norm"))
    in_maps = []
    for c in range(8):
        b, j = c // 4, c % 4
        v = vec.copy()
        kchunks, vchunks = [], []
        for i, src in enumerate([j, j - 1, j - 2, j - 3]):
            if src >= 0:
                kv = ra[4 * b + src]["kvT_out"]
                kchunks.append(kv[:D, :TP])
                vchunks.append(kv[D:, :TP])
            else:
                kchunks.append(np.zeros((D, TP), np.float32))
                vchunks.append(np.zeros((D, TP), np.float32))
                v[:, VB["sbias"] + i - 1] = NEG
        kT = np.concatenate(kchunks, axis=1)
        vT = np.concatenate(vchunks, axis=1)
        kp = np.ascontiguousarray(kT.reshape(NH, 128, 4096))
        vp = np.ascontiguousarray(vT.reshape(NH, 128, 32, 128).transpose(0, 3, 2, 1)).reshape(NH, 128, 4096)
        ks = np.zeros((8, 128, 4, 1088), np.float32)
        vs = np.zeros((8, 128, 4, 9, 128), np.float32)
        kvc = ra[c]["kvT_out"]
        for seq in range(2):
            sid = 2 * c + seq
            kfull = np.concatenate([ck[sid].transpose(1, 2, 0),
                                    kvc[:D, TP + 64 * seq:TP + 64 * (seq + 1)].reshape(NH, 128, 64)], axis=2)
            vfull = np.zeros((NH, 1152, 128), np.float32)
            vfull[:, :1024] = cv[sid].transpose(1, 0, 2)
            vfull[:, 1024:1088] = kvc[D:, TP + 64 * seq:TP + 64 * (seq + 1)].reshape(NH, 128, 64).transpose(0, 2, 1)
            for g4 in range(4):
                u = seq * 4 + g4
                ks[u] = kfull[4 * g4:4 * g4 + 4].transpose(1, 0, 2)
                vs[u] = vfull[4 * g4:4 * g4 + 4].reshape(4, 9, 128, 128).transpose(2, 0, 1, 3)
        m = dict(shared)
        m.update({"xT_in": ra[c]["xT_out"], "vecs": v, "kp": kp, "vp": vp,
                  "ks": ks.reshape(8, 128, KSLOT), "vs": vs.reshape(8, 128, VSLOT)})
        in_maps.append(m)
    nc = build_B()
    res = run_bass_kernel_spmd(nc, in_maps, core_ids=list(range(8)))
    return [{k: np.asarray(v) for k, v in r.items()} for r in res.results]


def kernel(**inp):
    ra = run_A(inp)
    rb = run_B(inp, ra)
    B, S = 2, 4096
    k_p = np.zeros((B, S, 16, 128), np.float32)
    v_p = np.zeros((B, S, 16, 128), np.float32)
    k_s = np.zeros((16, 64, 16, 128), np.float32)
    v_s = np.zeros((16, 64, 16, 128), np.float32)
    st_p = np.zeros((1, B, 30, D), np.float32)
    st_s = np.zeros((1, 16, 30, D), np.float32)
    y_p = np.zeros((B, S, D), np.float32)
    y_s = np.zeros((16, 64, D), np.float32)
    for c in range(8):
        b, j = c // 4, c % 4
        kv = ra[c]["kvT_out"]
        kT, vT = kv[:D], kv[D:]
        k_p[b, j * TP:(j + 1) * TP] = kT[:, :TP].T.reshape(TP, 16, 128)
        v_p[b, j * TP:(j + 1) * TP] = vT[:, :TP].T.reshape(TP, 16, 128)
        yT = rb[c]["yT_out"]
        y_p[b, j * TP:(j + 1) * TP] = yT[:, :TP].T
        for i in range(2):
            k_s[2 * c + i] = kT[:, TP + 64 * i:TP + 64 * (i + 1)].T.reshape(64, 16, 128)
            v_s[2 * c + i] = vT[:, TP + 64 * i:TP + 64 * (i + 1)].T.reshape(64, 16, 128)
            st_s[0, 2 * c + i] = ra[c]["sts_out"][:, 30 * i:30 * (i + 1)].T
            y_s[2 * c + i] = yT[:, TP + 64 * i:TP + 64 * (i + 1)].T
        if j == 3:
            st_p[0, b] = ra[c]["stp_out"].T
    return (y_p, y_s, st_p, k_p, v_p, st_s, k_s, v_s)
```
